# Optimizing a Trainium2 kernel written in Bass

```python
import jax, jax.numpy as jnp
from jax import lax
import numpy as np

D_MODEL = 1024
BATCH = 8
SEQ = 2048
DEPTH = 1

CTX_LEN = 256
GRID_W = 64
EPS = 1e-6

SSD_HEAD_DIM = 64
D_SSM = D_MODEL // 2
SSD_HEADS = D_SSM // SSD_HEAD_DIM
SSD_GROUPS = 2
SSD_STATE = 128
SSD_CONV = 5
SSD_CHUNK = 128
D_XB = D_SSM + SSD_GROUPS * SSD_STATE
D_XBC = D_SSM + 2 * SSD_GROUPS * SSD_STATE

POOL_WINDOWS = (2, 4, 8, 16)
POOL_GROUPS = len(POOL_WINDOWS)
D_POOL = D_MODEL // 2
POOL_GROUP_DIM = D_POOL // POOL_GROUPS
D_MIX = D_SSM + D_POOL

OFF_XBC = D_SSM
OFF_DT = OFF_XBC + D_XBC
OFF_POOL = OFF_DT + 2 * SSD_HEADS
D_IN = OFF_POOL + D_POOL

D_FF = ((8 * D_MODEL // 3 + 127) // 128) * 128
FFN_CONV = 3

kernel_name = "hybrid_ssd_pool_prefix_dit_block"


def rmsnorm(x, g):
    xf = x.astype(jnp.float32)
    y = xf * lax.rsqrt(jnp.mean(xf * xf, axis=-1, keepdims=True) + EPS)
    return (y * g.astype(jnp.float32)).astype(x.dtype)


def modulate(x, shift, scale):
    return x * (1 + scale) + shift


def dwconv(x, w, b):
    k = w.shape[0]
    y = lax.conv_general_dilated(
        x, w[:, None, :].astype(x.dtype), window_strides=(1,),
        padding=[(k // 2, k // 2)], dimension_numbers=("NWC", "WIO", "NWC"),
        feature_group_count=x.shape[-1])
    return y + b.astype(x.dtype)


def segsum_exp(cs):
    t = cs.shape[-1]
    diff = cs[..., :, None] - cs[..., None, :]
    mask = jnp.tril(jnp.ones((t, t), dtype=bool))
    return jnp.exp(jnp.where(mask, diff, -jnp.inf))


def ssd_dt(dt_raw, dt_bias, a_log):
    b, n = dt_raw.shape[:2]
    dt = jax.nn.softplus(dt_raw.astype(jnp.float32).reshape(b, n, 2, SSD_HEADS) + dt_bias.astype(jnp.float32))
    a = -jnp.exp(a_log.astype(jnp.float32))
    return dt, a


def ssd_chunked(xs, dt, a, bm, cm, h0):
    b, l, h, p = xs.shape
    g, n = bm.shape[-2:]
    k = h // g
    L = SSD_CHUNK
    c = l // L
    a_cs = jnp.cumsum((dt * a).reshape(b, c, L, h).transpose(0, 3, 1, 2), axis=-1)
    xd = (xs * dt[..., None]).reshape(b, c, L, g, k, p)
    bc = bm.reshape(b, c, L, g, n)
    cc = cm.reshape(b, c, L, g, n)
    decay_in = segsum_exp(a_cs).reshape(b, g, k, c, L, L)
    cb = jnp.einsum("bclgn,bcsgn->bcgls", cc, bc)
    y_diag = jnp.einsum("bcgls,bgkcls,bcsgkp->bclgkp", cb, decay_in, xd)
    decay_st = jnp.exp(a_cs[..., -1:] - a_cs).reshape(b, g, k, c, L)
    st = jnp.einsum("bclgn,bgkcl,bclgkp->bcgkpn", bc, decay_st, xd).reshape(b, c, h, p, n)
    st = jnp.concatenate([h0[:, None].astype(jnp.float32), st], axis=1)
    a_chunk = jnp.pad(a_cs[..., -1], ((0, 0), (0, 0), (1, 0)))
    decay_ch = segsum_exp(jnp.cumsum(a_chunk, axis=-1))
    st = jnp.einsum("bhzc,bchpn->bzhpn", decay_ch, st)
    prev, final = st[:, :-1], st[:, -1]
    decay_out = jnp.exp(a_cs).reshape(b, g, k, c, L)
    y_off = jnp.einsum("bclgn,bcgkpn,bgkcl->bclgkp", cc, prev.reshape(b, c, g, k, p, n), decay_out)
    return (y_diag + y_off).reshape(b, l, h, p), final


def ssd_final_state(xs, dt, a, bm):
    b, l, h, p = xs.shape
    g, n = bm.shape[-2:]
    k = h // g
    a_cs = jnp.cumsum(dt * a, axis=1)
    w = jnp.exp(a_cs[:, -1:] - a_cs) * dt
    st = jnp.einsum("blgn,blgk,blgkp->bgkpn", bm, w.reshape(b, l, g, k), xs.reshape(b, l, g, k, p))
    return st.reshape(b, h, p, n)


def box_mean(x, w, axis):
    n = x.shape[axis]
    cs = jnp.cumsum(x, axis=axis)
    pad = [(0, 0)] * x.ndim
    pad[axis] = (1, 0)
    cs = jnp.pad(cs, pad)
    t = np.arange(n)
    lo = np.clip(t - w // 2, 0, n)
    hi = np.clip(t + w - w // 2, 0, n)
    shape = [1] * x.ndim
    shape[axis] = n
    cnt = jnp.asarray((hi - lo).astype(np.float32).reshape(shape))
    return (jnp.take(cs, jnp.asarray(hi), axis=axis) - jnp.take(cs, jnp.asarray(lo), axis=axis)) / cnt


def pool_mixer(u, w_pool, pool_scale, rows):
    b, n = u.shape[:2]
    ug = u.reshape(b, n, POOL_GROUPS, POOL_GROUP_DIM)
    outs = []
    for gi, w in enumerate(POOL_WINDOWS):
        v = ug[:, :, gi]
        if rows is None:
            m = box_mean(v, w, 1)
        else:
            v2 = v.reshape(b, rows, GRID_W, POOL_GROUP_DIM)
            m = box_mean(box_mean(v2, w, 1), w, 2).reshape(b, n, POOL_GROUP_DIM)
        outs.append(m - v)
    d = jnp.stack(outs, axis=2)
    y = jnp.einsum("bngc,gcd->bngd", d, w_pool.astype(jnp.float32)).reshape(b, n, D_POOL)
    return y * pool_scale.astype(jnp.float32)


def token_mixers(h, w_in, conv_w, conv_b, dt_bias, a_log, d_skip, ssm_g, w_pool, pool_scale, w_out,
                 rows, h0_f, h0_b):
    b, n = h.shape[:2]
    proj = h @ w_in
    z = proj[..., :OFF_XBC].astype(jnp.float32)
    xbc = jax.nn.silu(dwconv(proj[..., OFF_XBC:OFF_DT], conv_w, conv_b)).astype(jnp.float32)
    xs = xbc[..., :D_SSM].reshape(b, n, SSD_HEADS, SSD_HEAD_DIM)
    bm = xbc[..., D_SSM:D_XB].reshape(b, n, SSD_GROUPS, SSD_STATE)
    cm = xbc[..., D_XB:].reshape(b, n, SSD_GROUPS, SSD_STATE)
    dt, a = ssd_dt(proj[..., OFF_DT:OFF_POOL], dt_bias, a_log)
    y_f, s_f = ssd_chunked(xs, dt[:, :, 0], a[0], bm, cm, h0_f)
    y_b, s_b = ssd_chunked(xs[:, ::-1], dt[:, ::-1, 1], a[1], bm[:, ::-1], cm[:, ::-1], h0_b)
    y = y_f + y_b[:, ::-1] + d_skip.astype(jnp.float32)[:, None] * xs
    y = rmsnorm(y.reshape(b, n, D_SSM) * jax.nn.silu(z), ssm_g)
    p = pool_mixer(proj[..., OFF_POOL:].astype(jnp.float32), w_pool, pool_scale, rows)
    mix = jnp.concatenate([y, p], axis=-1).astype(h.dtype)
    return mix @ w_out, s_f, s_b


def context_scan_states(hc, w_in, conv_w, conv_b, dt_bias, a_log):
    b, n = hc.shape[:2]
    xb = jax.nn.silu(dwconv(hc @ w_in[:, OFF_XBC:OFF_XBC + D_XB], conv_w[:, :D_XB], conv_b[:D_XB]))
    xb = xb.astype(jnp.float32)
    xs = xb[..., :D_SSM].reshape(b, n, SSD_HEADS, SSD_HEAD_DIM)
    bm = xb[..., D_SSM:].reshape(b, n, SSD_GROUPS, SSD_STATE)
    dt, a = ssd_dt(hc @ w_in[:, OFF_DT:OFF_POOL], dt_bias, a_log)
    s_f = ssd_final_state(xs, dt[:, :, 0], a[0], bm)
    s_b = ssd_final_state(xs[:, ::-1], dt[:, ::-1, 1], a[1], bm[:, ::-1])
    return s_f, s_b


def conv_ffn(h, w_up, conv_w, conv_b, w_down):
    up = dwconv(h @ w_up, conv_w, conv_b)
    v, g = jnp.split(up, 2, axis=-1)
    return (jax.nn.silu(g) * v) @ w_down


def setup_inputs(seed: int = 0) -> dict:
    key = jax.random.key(seed)
    ks = jax.random.split(key, 26)
    f = jnp.float32
    L = DEPTH

    def nrm(k, shape, scale):
        return jax.random.normal(k, shape, f) * scale

    def gain(k, shape):
        return 1.0 + 0.02 * jax.random.normal(k, shape, f)

    dt0 = jnp.exp(jax.random.uniform(ks[14], (L, 2, SSD_HEADS), f, float(np.log(1e-3)), float(np.log(1e-1))))
    return {
        "x": nrm(ks[0], (BATCH, SEQ, D_MODEL), 1.0),
        "c": nrm(ks[1], (BATCH, D_MODEL), 1.0),
        "ctx": nrm(ks[2], (BATCH, CTX_LEN, D_MODEL), 1.0),
        "c_ctx": nrm(ks[3], (D_MODEL,), 1.0),
        "pre_norm_mix": gain(ks[4], (L, D_MODEL)),
        "post_norm_mix": gain(ks[5], (L, D_MODEL)),
        "pre_norm_ffn": gain(ks[6], (L, D_MODEL)),
        "post_norm_ffn": gain(ks[7], (L, D_MODEL)),
        "w_ada": nrm(ks[8], (L, D_MODEL, 6 * D_MODEL), 0.5 * D_MODEL ** -0.5),
        "b_ada": nrm(ks[9], (L, 6 * D_MODEL), 0.01),
        "w_in": nrm(ks[10], (L, D_MODEL, D_IN), D_MODEL ** -0.5),
        "conv_ssd_w": nrm(ks[11], (L, SSD_CONV, D_XBC), SSD_CONV ** -0.5),
        "conv_ssd_b": nrm(ks[12], (L, D_XBC), 0.01),
        "dt_bias": dt0 + jnp.log(-jnp.expm1(-dt0)),
        "a_log": jnp.log(jax.random.uniform(ks[13], (L, 2, SSD_HEADS), f, 1.0, 16.0)),
        "d_skip": 1.0 + 0.1 * jax.random.normal(ks[15], (L, SSD_HEADS), f),
        "ssm_norm": gain(ks[16], (L, D_SSM)),
        "w_pool": nrm(ks[17], (L, POOL_GROUPS, POOL_GROUP_DIM, POOL_GROUP_DIM), POOL_GROUP_DIM ** -0.5),
        "pool_scale": 1.0 + 0.1 * jax.random.normal(ks[18], (L, D_POOL), f),
        "w_out": nrm(ks[19], (L, D_MIX, D_MODEL), D_MIX ** -0.5),
        "w_up": nrm(ks[20], (L, D_MODEL, 2 * D_FF), D_MODEL ** -0.5),
        "conv_ffn_w": nrm(ks[21], (L, FFN_CONV, 2 * D_FF), FFN_CONV ** -0.5),
        "conv_ffn_b": nrm(ks[22], (L, 2 * D_FF), 0.01),
        "w_down": nrm(ks[23], (L, D_FF, D_MODEL), D_FF ** -0.5),
    }


def reference(x, c, ctx, c_ctx, pre_norm_mix, post_norm_mix, pre_norm_ffn, post_norm_ffn, w_ada, b_ada,
              w_in, conv_ssd_w, conv_ssd_b, dt_bias, a_log, d_skip, ssm_norm, w_pool, pool_scale, w_out,
              w_up, conv_ffn_w, conv_ffn_b, w_down):
    b = x.shape[0]
    rows = x.shape[1] // GRID_W
    for i in range(DEPTH):
        mx = jnp.split((jax.nn.silu(c) @ w_ada[i] + b_ada[i])[:, None, :], 6, axis=-1)
        mc = jnp.split(jax.nn.silu(c_ctx) @ w_ada[i] + b_ada[i], 6, axis=-1)

        hc = modulate(rmsnorm(ctx, pre_norm_mix[i]), mc[0], mc[1])
        if i < DEPTH - 1:
            zeros = jnp.zeros((b, SSD_HEADS, SSD_HEAD_DIM, SSD_STATE), jnp.float32)
            yc, s_f, s_b = token_mixers(hc, w_in[i], conv_ssd_w[i], conv_ssd_b[i], dt_bias[i], a_log[i],
                                        d_skip[i], ssm_norm[i], w_pool[i], pool_scale[i], w_out[i],
                                        None, zeros, zeros)
            ctx = ctx + mc[2] * rmsnorm(yc, post_norm_mix[i])
            hc2 = modulate(rmsnorm(ctx, pre_norm_ffn[i]), mc[3], mc[4])
            ctx = ctx + mc[5] * rmsnorm(conv_ffn(hc2, w_up[i], conv_ffn_w[i], conv_ffn_b[i], w_down[i]),
                                        post_norm_ffn[i])
        else:
            s_f, s_b = context_scan_states(hc, w_in[i], conv_ssd_w[i], conv_ssd_b[i], dt_bias[i], a_log[i])

        hx = modulate(rmsnorm(x, pre_norm_mix[i]), mx[0], mx[1])
        yx, _, _ = token_mixers(hx, w_in[i], conv_ssd_w[i], conv_ssd_b[i], dt_bias[i], a_log[i],
                                d_skip[i], ssm_norm[i], w_pool[i], pool_scale[i], w_out[i],
                                rows, s_f, s_b)
        x = x + mx[2] * rmsnorm(yx, post_norm_mix[i])
        hx = modulate(rmsnorm(x, pre_norm_ffn[i]), mx[3], mx[4])
        x = x + mx[5] * rmsnorm(conv_ffn(hx, w_up[i], conv_ffn_w[i], conv_ffn_b[i], w_down[i]),
                                post_norm_ffn[i])
    return x
```

```python
import numpy as np
import concourse.bass as bass
import concourse.mybir as mybir
from concourse.bass_utils import run_bass_kernel_spmd

F32 = mybir.dt.float32
BF16 = mybir.dt.bfloat16
AF = mybir.ActivationFunctionType
ALU = mybir.AluOpType

D = 1024
SEQ = 2048
NT = 16
CTX = 256
NCT = 2
D_IN = 2064
D_FF = 2816
NFF = 22
EPS = 1e-6
GRID_W = 64
ROWS = 32
POOL_W = (2, 4, 8, 16)

ENGS = ("pe", "act", "dve", "pool", "sp")


class Tok:
    __slots__ = ("name", "w", "rs", "drs")

    def __init__(self, name):
        self.name = name
        self.w = None
        self.rs = {}
        self.drs = []


class Op:
    __slots__ = ("eng", "fn", "deps", "sig", "val", "dma", "dsem", "dval", "name", "bar")


class Prog:
    def __init__(self, nc, n_dma_sems=12):
        self.nc = nc
        self.ops = []
        self.eng_ops = {e: [] for e in ENGS}
        self.last = {e: None for e in ENGS}
        self.pending_bar = {e: [] for e in ENGS}
        self.dma_ops = []
        self.n_dma_sems = n_dma_sems
        self.dma_rr = {e: 0 for e in ENGS}
        self.dma_last_on_sem = {}

    def toks(self, name, n):
        return [Tok(f"{name}{i}") for i in range(n)]

    def inherit(self, new, olds):
        for o in olds:
            if o.w is not None:
                new.drs.append(o.w)
            for r in o.rs.values():
                new.drs.append(r)
            new.drs.extend(o.drs)

    def op(self, eng, fn, reads=(), writes=(), dma=False, name="", bar=True):
        o = Op()
        o.eng, o.fn, o.dma, o.name, o.bar = eng, fn, dma, name, bar
        o.sig, o.val, o.dsem, o.dval = False, 0, None, 0
        deps = []
        raw = set()
        for t in reads:
            if t.w is not None:
                deps.append(t.w)
                raw.add(id(t.w))
        for t in writes:
            if t.w is not None:
                deps.append(t.w)
            deps.extend(t.rs.values())
            deps.extend(t.drs)
        deps.extend(self.pending_bar[eng])
        self.pending_bar[eng] = []
        if dma:
            k = (eng, self.dma_rr[eng] % self.n_dma_sems)
            self.dma_rr[eng] += 1
            prev = self.dma_last_on_sem.get(k)
            if prev is not None:
                deps.append(prev)
                o.dval = prev.dval + 16
            else:
                o.dval = 16
            o.dsem = k
            self.dma_last_on_sem[k] = o
            self.dma_ops.append(o)
        fd = []
        seen = set()
        for d in deps:
            if d is o or id(d) in seen:
                continue
            seen.add(id(d))
            if (not d.dma) and (not dma) and d.eng == eng:
                if eng == "pe" or id(d) not in raw:
                    continue
            fd.append(d)
            if not d.dma:
                d.sig = True
        o.deps = fd
        for t in reads:
            if dma:
                t.drs.append(o)
            else:
                t.rs[eng] = o
        for t in writes:
            t.w = o
            t.rs = {}
            t.drs = []
        self.ops.append(o)
        self.eng_ops[eng].append(o)
        self.last[eng] = o
        return o

    def barrier(self):
        lasts = [self.last[e] for e in ENGS if self.last[e] is not None and not self.last[e].dma]
        dmas = [d for d in self.dma_ops if d.bar]
        self.dma_ops = [d for d in self.dma_ops if not d.bar]
        for e in ENGS:
            self.pending_bar[e] = list(lasts) + list(dmas)
            for d in lasts:
                if d.eng != e:
                    d.sig = True

    def emit(self, final_waits):
        nc = self.nc
        for e in ENGS:
            c = 0
            for o in self.eng_ops[e]:
                if o.sig:
                    c += 1
                    o.val = c
        from contextlib import ExitStack
        with ExitStack() as st:
            esem = {e: st.enter_context(nc.semaphore(f"s_{e}")) for e in ENGS}
            dsem = {}
            for k in self.dma_last_on_sem:
                dsem[k] = st.enter_context(nc.semaphore(f"d_{k[0]}{k[1]}"))
            block = st.enter_context(nc.Block())

            def run(ename, eng):
                waited = {}
                for o in self.eng_ops[ename]:
                    for d in o.deps:
                        if d.dma:
                            key, sem, val = ("d", d.dsem), dsem[d.dsem], d.dval
                        else:
                            key, sem, val = ("e", d.eng), esem[d.eng], d.val
                        if waited.get(key, 0) >= val:
                            continue
                        waited[key] = val
                        eng.wait_ge(sem, val)
                    ins = o.fn(eng)
                    if o.dma:
                        ins.then_inc(dsem[o.dsem], 16)
                    elif o.sig:
                        ins.then_inc(esem[ename], 1)
                if ename == "sp":
                    for d in final_waits:
                        eng.wait_ge(dsem[d.dsem], d.dval)

            block.tensor(lambda eng: run("pe", eng))
            block.scalar(lambda eng: run("act", eng))
            block.vector(lambda eng: run("dve", eng))
            block.gpsimd(lambda eng: run("pool", eng))
            block.sync(lambda eng: run("sp", eng))


def _pool_tables():
    mats = []
    index = {}
    table = {}
    inv_cnt = np.zeros((128, NT, 4), np.float32)
    t = np.arange(SEQ)
    r, c = t // GRID_W, t % GRID_W
    for gi, w in enumerate(POOL_W):
        lo_r = np.clip(r - w // 2, 0, ROWS)
        hi_r = np.clip(r + w - w // 2, 0, ROWS)
        lo_c = np.clip(c - w // 2, 0, GRID_W)
        hi_c = np.clip(c + w - w // 2, 0, GRID_W)
        cnt = (hi_r - lo_r) * (hi_c - lo_c)
        inv_cnt[:, :, gi] = (1.0 / cnt.astype(np.float64)).astype(np.float32).reshape(NT, 128).T
        for io in range(NT):
            to = np.arange(io * 128, (io + 1) * 128)
            for ii in range(NT):
                ti = np.arange(ii * 128, (ii + 1) * 128)
                m = ((r[ti][:, None] >= lo_r[to][None, :]) & (r[ti][:, None] < hi_r[to][None, :]) &
                     (c[ti][:, None] >= lo_c[to][None, :]) & (c[ti][:, None] < hi_c[to][None, :]))
                m = m.astype(np.float32)
                if io == ii:
                    m = m - np.diag(cnt[to].astype(np.float32))
                if not m.any():
                    continue
                key = m.tobytes()
                if key not in index:
                    index[key] = len(mats)
                    mats.append(m)
                table[(gi, io, ii)] = index[key]
    return np.stack(mats, axis=1), table, inv_cnt


_POOL_MATS, _POOL_TABLE, _INV_CNT = _pool_tables()
NM = _POOL_MATS.shape[1]


def _tri():
    j = np.arange(128)[:, None]
    l = np.arange(128)[None, :]
    out = np.stack([(j <= l), (j > l), (j < l), (j >= l)], axis=1).astype(np.float32)
    return out


VO = {}
_off = 0
for _n, _w in [("cc", 16), ("b_ada", 48), ("g_pre_mix", 8), ("g_post_mix", 8), ("g_pre_ffn", 8),
               ("g_post_ffn", 8), ("cw_ssd", 40), ("cb_ssd", 8), ("cw_ffn", 132), ("cb_ffn", 44),
               ("dt_bias", 16), ("a_log", 16), ("d_skip", 8), ("inv_cnt", 64), ("ssm_norm", 512),
               ("pool_scale", 512), ("neghalf", 32), ("ones", 128)]:
    VO[_n] = (_off, _w)
    _off += _w
NV = _off


def _fm(v, n):
    return np.ascontiguousarray(v.reshape(n, 128).T)


def _host_vecs(inp, b):
    v = np.zeros((128, NV), np.float32)

    def put(name, arr):
        o, w = VO[name]
        v[:, o:o + w] = arr.reshape(128, w)

    cc = np.stack([_fm(inp["c"][b], 8), _fm(inp["c_ctx"], 8)], axis=2)
    put("cc", cc)
    put("b_ada", _fm(inp["b_ada"][0], 48))
    put("g_pre_mix", _fm(inp["pre_norm_mix"][0], 8))
    put("g_post_mix", _fm(inp["post_norm_mix"][0], 8))
    put("g_pre_ffn", _fm(inp["pre_norm_ffn"][0], 8))
    put("g_post_ffn", _fm(inp["post_norm_ffn"][0], 8))
    put("cw_ssd", np.ascontiguousarray(inp["conv_ssd_w"][0].reshape(5, 8, 128).transpose(2, 1, 0)))
    put("cb_ssd", _fm(inp["conv_ssd_b"][0], 8))
    put("cw_ffn", np.ascontiguousarray(inp["conv_ffn_w"][0].reshape(3, 44, 128).transpose(2, 1, 0)))
    put("cb_ffn", _fm(inp["conv_ffn_b"][0], 44))
    put("dt_bias", np.broadcast_to(inp["dt_bias"][0].reshape(1, 16), (128, 16)))
    put("a_log", np.broadcast_to(inp["a_log"][0].reshape(1, 16), (128, 16)))
    put("d_skip", np.broadcast_to(inp["d_skip"][0].reshape(1, 8), (128, 8)))
    put("inv_cnt", _INV_CNT)
    put("ssm_norm", np.broadcast_to(inp["ssm_norm"][0].reshape(1, 512), (128, 512)))
    put("pool_scale", np.broadcast_to(inp["pool_scale"][0].reshape(1, 512), (128, 512)))
    put("neghalf", np.full((128, 32), -0.5, np.float32))
    put("ones", np.ones((128, 128), np.float32))
    return v


def build_nc(debug=(), stop_after=None):
    nc = bass.Bass("TRN2", target_bir_lowering=False)

    def din(name, shape, dt=F32):
        return nc.dram_tensor(name, list(shape), dt, kind="ExternalInput").ap()

    x_d = din("x", [SEQ, D])
    ctx_d = din("ctx", [CTX, D])
    vecs_d = din("vecs", [128, NV])
    w_ada_d = din("w_ada", [D, 6 * D])
    w_in_d = din("w_in", [D, D_IN])
    w_pool_d = din("w_pool", [4, 128, 128])
    w_out_d = din("w_out", [D, D])
    w_up_d = din("w_up", [D, 2 * D_FF])
    w_down_d = din("w_down", [D_FF, D])
    ident_d = din("ident", [128, 128])
    tri_d = din("tri", [128, 4, 128])
    pmat_d = din("pmat", [128, NM, 128])
    out_d = nc.dram_tensor("out", [SEQ, D], F32, kind="ExternalOutput").ap()
    dbg_d = {}
    for name, shape, dt in debug:
        dbg_d[name] = nc.dram_tensor("dbg_" + name, list(shape), dt, kind="ExternalOutput").ap()

    from contextlib import ExitStack
    with ExitStack() as es:
        ARENA_BYTES = 212000
        arena = es.enter_context(nc.sbuf_tensor("arena", [128, ARENA_BYTES // 4], F32))
        psum = es.enter_context(nc.psum_tensor("psum", [128, 8, 512], F32))
        P = Prog(nc)
        PSB = P.toks("psb", 8)
        final_waits = []

        def carve(off_bytes, shape, dt):
            esz = 4 if dt == F32 else 2
            n = int(np.prod(shape[1:]))
            assert off_bytes % 4 == 0
            a = arena[:, off_bytes // 4: off_bytes // 4 + (n * esz + 3) // 4]
            if dt != F32:
                a = a.bitcast(dt)
                a = a[:, 0:n]
            if len(shape) == 3:
                a = a.rearrange("p (a b) -> p a b", b=shape[2])
            elif len(shape) == 4:
                a = a.rearrange("p (a b c) -> p a b c", b=shape[2], c=shape[3])
            return a

        class Bump:
            def __init__(self, start):
                self.off = start

            def get(self, shape, dt):
                esz = 4 if dt == F32 else 2
                n = int(np.prod(shape[1:])) * esz
                n = (n + 31) // 32 * 32
                v = carve(self.off, shape, dt)
                self.off += n
                assert self.off <= ARENA_BYTES, (self.off, ARENA_BYTES)
                return v

        def ps_f32(bank, n=512, off=0):
            return psum[:, bank, off:off + n]

        def ps_bf16(bank):
            return psum[:, bank, :].bitcast(BF16)

        G = Bump(0)
        vecs = G.get([128, NV], F32)
        ident_f = G.get([128, 128], F32)
        ident_b = G.get([128, 128], BF16)
        tri_f = G.get([128, 4, 128], F32)
        tri_b = G.get([128, 4, 128], BF16)
        pmat = G.get([128, NM, 128], BF16)
        sc = G.get([128, 8, 2], BF16)
        ada = G.get([128, 48, 2], F32)
        mod = G.get([128, 8, 8], F32)
        gg_rep = G.get([128, 2, 1024], F32)
        ss = G.get([128, 64], F32)
        rstd = G.get([128, 64], F32)
        G_END = G.off
        T_vecs, T_const, T_sc, T_ada, T_mod, T_gg = (Tok("vecs"), Tok("const"), Tok("sc"), Tok("ada"),
                                                     Tok("mod"), Tok("gg"))

        def V(name):
            o, w = VO[name]
            return vecs[:, o:o + w]

        P.op("sp", lambda e: e.dma_start(out=vecs, in_=vecs_d), writes=[T_vecs], dma=True)
        P.op("sp", lambda e: e.dma_start(out=ident_f, in_=ident_d), writes=[T_const], dma=True)
        P.op("sp", lambda e: e.dma_start(out=tri_f, in_=tri_d), writes=[T_const], dma=True)
        P.op("pool", lambda e: e.dma_start(out=ident_b, in_=ident_d), writes=[T_const], dma=True)
        P.op("pool", lambda e: e.dma_start(out=tri_b, in_=tri_d), writes=[T_const], dma=True)
        P.op("pool", lambda e: e.dma_start(out=pmat, in_=pmat_d), writes=[T_const], dma=True)

        M1 = Bump(G_END)
        hT = M1.get([128, 8, SEQ], BF16)
        A1_OFF, A1_END = G_END, M1.off
        w_in = M1.get([128, 8, D_IN], BF16)
        A2_OFF, A2_END = A1_END, M1.off
        xbcT = M1.get([128, 8, SEQ], BF16)
        A3_OFF = A2_END
        zs = M1.get([128, NT, 512], BF16)
        dtr = M1.get([128, NT + NCT, 16], F32)
        xbc_ctx = M1.get([128, 6, CTX], BF16)
        WA_OFF = M1.off
        wa = [M1.get([128, 8, 256], BF16) for _ in range(4)]
        VT_OFF = M1.off
        v_tok = M1.get([128, NT, 512], BF16)
        HC_OFF = M1.off
        hcT = M1.get([128, 8, CTX], BF16)
        S1_OFF = M1.off
        xt = [M1.get([128, D], F32) for _ in range(3)]
        xh = [M1.get([128, D], BF16) for _ in range(2)]
        junk = M1.get([128, D], BF16)
        S1_END = M1.off
        T_hT = P.toks("hT", NT)
        T_hcT = P.toks("hcT", NCT)
        T_win = P.toks("win", 4)
        T_wa = P.toks("wa", 4)
        T_xt = P.toks("xt", 3)
        T_xh = P.toks("xh", 2)
        T_junk = Tok("junk")
        T_ss = P.toks("ss", 64)

        w_in_v = w_in_d.rearrange("(k p) n -> p k n", p=128)
        WIN_PIECES = [(0, 512), (512, 1024), (1024, 1552), (1552, 2064)]

        def load_w_in():
            for pi in (0, 3, 2, 1):
                c0, c1 = WIN_PIECES[pi]
                P.op("pool", lambda e, c0=c0, c1=c1: e.dma_start(out=w_in[:, :, c0:c1], in_=w_in_v[:, :, c0:c1]),
                     writes=[T_win[pi]], dma=True)

        P.op("act", lambda e: e.activation(out=sc, in_=V("cc").rearrange("p (a b) -> p a b", b=2), func=AF.Silu),
             reads=[T_vecs], writes=[T_sc])
        w_ada_v = w_ada_d.rearrange("(k p) n -> p k n", p=128)
        ADA_BANK = 7

        wa8 = [carve(A3_OFF + q * 4096, [128, 8, 256], BF16) for q in range(8)]
        T_wa8 = P.toks("wa8", 8)

        def ada_dma(s, buf, tok):
            P.op("pool", lambda e: e.dma_start(out=buf, in_=w_ada_v[:, :, s * 256:(s + 1) * 256]),
                 writes=[tok], dma=True)

        def ada_mm(s, buf, tok):
            for q in range(2):
                cch = 2 * s + q
                for k in range(8):
                    P.op("pe", lambda e, q=q, k=k, cch=cch: e.matmul(
                        psum[:, ADA_BANK, cch * 2:cch * 2 + 2], lhsT=buf[:, k, q * 128:(q + 1) * 128],
                        rhs=sc[:, k, :], start=(k == 0), stop=(k == 7)),
                        reads=[tok, T_sc], writes=[PSB[ADA_BANK]])

        def ada_fin(s0, s1):
            c0, c1 = 2 * s0, 2 * s1
            bo = VO["b_ada"][0]
            P.op("dve", lambda e: e.tensor_tensor(
                out=ada[:, c0:c1, :], in0=psum[:, ADA_BANK, 2 * c0:2 * c1].rearrange("p (a b) -> p a b", b=2),
                in1=vecs[:, bo + c0:bo + c1].unsqueeze(2).to_broadcast([128, c1 - c0, 2]), op=ALU.add),
                reads=[PSB[ADA_BANK], T_vecs], writes=[T_ada])

        def mod_vec(dst, scale_c0, shift_c0, gname, which):
            P.op("dve", lambda e: e.tensor_scalar(out=mod[:, :, dst], in0=ada[:, scale_c0:scale_c0 + 8, which],
                                                  scalar1=1.0, scalar2=None, op0=ALU.add),
                 reads=[T_ada], writes=[T_mod])
            P.op("dve", lambda e: e.tensor_tensor(out=mod[:, :, dst], in0=mod[:, :, dst], in1=V(gname), op=ALU.mult),
                 reads=[T_vecs, T_mod], writes=[T_mod])
            P.op("dve", lambda e: e.tensor_copy(out=mod[:, :, dst + 1], in_=ada[:, shift_c0:shift_c0 + 8, which]),
                 reads=[T_ada], writes=[T_mod])

        for s_ in range(8):
            ada_dma(s_, wa8[s_], T_wa8[s_])
        load_w_in()
        for s_ in range(8):
            ada_mm(s_, wa8[s_], T_wa8[s_])
        ada_fin(0, 8)
        mod_vec(0, 8, 0, "g_pre_mix", 0)
        mod_vec(2, 8, 0, "g_pre_mix", 1)

        PT_BANKS = [5, 6]
        nrm_cnt = [0]

        def norm_tile(src_rows, col, gm_i, dstT, dcols, T_dst, load=True, xt_slot=None):
            i = nrm_cnt[0]
            nrm_cnt[0] += 1
            s3, s2 = i % 3, i % 2
            P.op("sp", lambda e: e.dma_start(out=xt[s3], in_=src_rows), writes=[T_xt[s3]], dma=True)
            P.op("act", lambda e: e.activation(out=junk, in_=xt[s3], func=AF.Square, accum_out=ss[:, col:col + 1]),
                 reads=[T_xt[s3]], writes=[T_junk, T_ss[col]])
            P.op("pool", lambda e: e.tensor_scalar(out=ss[:, col:col + 1], in0=ss[:, col:col + 1], scalar1=1.0 / D,
                                                   scalar2=EPS, op0=ALU.mult, op1=ALU.add),
                 reads=[T_ss[col]], writes=[T_ss[col]])
            P.op("pool", lambda e: e.tensor_tensor(out=rstd[:, col:col + 1], in0=ss[:, col:col + 1],
                                                   in1=V("neghalf")[:, 0:1], op=ALU.pow),
                 reads=[T_ss[col], T_vecs], writes=[T_ss[col]])
            P.op("act", lambda e: e.activation(out=xh[s2], in_=xt[s3], func=AF.Copy, scale=rstd[:, col:col + 1]),
                 reads=[T_xt[s3], T_ss[col]], writes=[T_xh[s2]])
            bank = PT_BANKS[s2]
            pt = ps_bf16(bank).rearrange("p (a b) -> p a b", b=128)
            for j in range(8):
                P.op("pe", lambda e, j=j: e.transpose(out=pt[:, j, :], in_=xh[s2][:, j * 128:(j + 1) * 128],
                                                      identity=ident_b),
                     reads=[T_xh[s2], T_const], writes=[PSB[bank]])
            for j in range(8):
                P.op("dve", lambda e, j=j: e.tensor_scalar(out=dstT[:, j, dcols], in0=pt[:, j, :],
                                                           scalar1=mod[:, j, gm_i:gm_i + 1],
                                                           scalar2=mod[:, j, gm_i + 1:gm_i + 2],
                                                           op0=ALU.mult, op1=ALU.add),
                     reads=[PSB[bank], T_mod], writes=[T_dst])

        for i in range(NCT):
            norm_tile(ctx_d[i * 128:(i + 1) * 128, :], 16 + i, 2, hcT, slice(i * 128, (i + 1) * 128), T_hcT[i])
        for i in range(NT):
            norm_tile(x_d[i * 128:(i + 1) * 128, :], i, 0, hT, slice(i * 128, (i + 1) * 128), T_hT[i])
            if i == 11:
                for s_ in range(8, 12):
                    ada_dma(s_, wa[s_ % 4], T_wa[s_ % 4])


        T_zs = P.toks("zs", NT)
        T_vtok = P.toks("vtok", NT)
        T_dtr = Tok("dtr")
        T_xbc = P.toks("xbc", 8)
        for j in range(8):
            P.inherit(T_xbc[j], [T_wa8[j]])
        T_xbcc = Tok("xbcc")
        cw_ssd = V("cw_ssd").rearrange("p (j k) -> p j k", k=5)
        cb_ssd = V("cb_ssd")

        def win_tok(c0):
            for pi, (a, b) in enumerate(WIN_PIECES):
                if a <= c0 < b:
                    return T_win[pi]

        def c1_tile(i, srcT, cols, T_src, slot, only_dt):
            zb, pb = slot, 2 + slot
            dp = psum[:, 4, slot * 16:(slot + 1) * 16]
            for k in range(8):
                lhsT = srcT[:, k, cols]
                if not only_dt:
                    P.op("pe", lambda e, k=k, lhsT=lhsT: e.matmul(ps_f32(zb), lhsT=lhsT, rhs=w_in[:, k, 0:512],
                                                                  start=(k == 0), stop=(k == 7)),
                         reads=[T_src, T_win[0]], writes=[PSB[zb]])
                    P.op("pe", lambda e, k=k, lhsT=lhsT: e.matmul(ps_f32(pb), lhsT=lhsT, rhs=w_in[:, k, 1552:2064],
                                                                  start=(k == 0), stop=(k == 7)),
                         reads=[T_src, T_win[3]], writes=[PSB[pb]])
                P.op("pe", lambda e, k=k, lhsT=lhsT: e.matmul(dp, lhsT=lhsT, rhs=w_in[:, k, 1536:1552],
                                                              start=(k == 0), stop=(k == 7)),
                     reads=[T_src, T_win[2]], writes=[PSB[4]])
            if not only_dt:
                P.op("act", lambda e: e.activation(out=zs[:, i, :], in_=ps_f32(zb), func=AF.Silu),
                     reads=[PSB[zb]], writes=[T_zs[i]])
                P.op("dve", lambda e: e.tensor_copy(out=v_tok[:, i, :], in_=ps_f32(pb)),
                     reads=[PSB[pb]], writes=[T_vtok[i]])
            P.op("dve", lambda e: e.tensor_tensor(out=dtr[:, i, :], in0=dp, in1=V("dt_bias"), op=ALU.add),
                 reads=[PSB[4], T_vecs], writes=[T_dtr])

        for i in range(NCT):
            c1_tile(NT + i, hcT, slice(i * 128, (i + 1) * 128), T_hcT[i], i % 2, True)
        for i in range(NT):
            c1_tile(i, hT, slice(i * 128, (i + 1) * 128), T_hT[i], i % 2, False)
            s_ = 8 + i
            ada_mm(s_, wa[s_ % 4], T_wa[s_ % 4])
            if s_ + 4 < 24:
                ada_dma(s_ + 4, wa[(s_ + 4) % 4], T_wa[(s_ + 4) % 4])

        ada_fin(8, 24)
        mod_vec(4, 32, 24, "g_pre_ffn", 0)
        P.op("dve", lambda e: e.tensor_tensor(out=mod[:, :, 6], in0=ada[:, 16:24, 0], in1=V("g_post_mix"), op=ALU.mult),
             reads=[T_ada, T_vecs], writes=[T_mod])
        P.op("dve", lambda e: e.tensor_tensor(out=mod[:, :, 7], in0=ada[:, 40:48, 0], in1=V("g_post_ffn"), op=ALU.mult),
             reads=[T_ada, T_vecs], writes=[T_mod])
        dg = carve(WA_OFF, [128, 8, 128], F32)
        T_dg = Tok("dg")
        P.inherit(T_dg, T_wa)
        ones_f = V("ones")
        for idx in range(2):
            for j in range(8):
                P.op("dve", lambda e, j=j, idx=idx: e.tensor_scalar(out=dg[:, j, :], in0=ident_f,
                                                                    scalar1=mod[:, j, 6 + idx:7 + idx], scalar2=None,
                                                                    op0=ALU.mult),
                     reads=[T_const, T_mod], writes=[T_dg])
            for half in range(2):
                bank = 5 + half
                P.op("pe", lambda e, half=half, bank=bank: e.matmul(
                    ps_f32(bank), lhsT=ones_f, rhs=dg[:, 4 * half:4 * half + 4, :].rearrange("p a b -> p (a b)"),
                    start=True, stop=True), reads=[T_dg, T_vecs], writes=[PSB[bank]])
                P.op("act", lambda e, half=half, bank=bank, idx=idx: e.activation(
                    out=gg_rep[:, idx, half * 512:(half + 1) * 512], in_=ps_f32(bank), func=AF.Copy),
                    reads=[PSB[bank]], writes=[T_gg])


        acc = [carve(S1_OFF + q * 8192, [128, SEQ], F32) for q in range(2)]
        T_acc = P.toks("acc", 2)
        for q in range(2):
            P.inherit(T_acc[q], T_xt + T_xh + [T_junk])
        c2_cnt = [0]

        def c2_s1(idx, j, srcT, T_srcs, ntok, dst, T_dst):
            slot = idx % 2
            nq = (ntok + 511) // 512
            banks = [4 * slot + q for q in range(nq)]
            xp = psum[:, 4 * slot:4 * slot + 4, :].rearrange("p a b -> p (a b)")
            c0 = 512 + j * 128
            for k in range(8):
                for q in range(nq):
                    n = min(512, ntok - q * 512)
                    P.op("pe", lambda e, k=k, q=q, n=n: e.matmul(
                        psum[:, 4 * slot + q, 0:n], lhsT=w_in[:, k, c0:c0 + 128],
                        rhs=srcT[:, k, q * 512:q * 512 + n], start=(k == 0), stop=(k == 7)),
                        reads=[win_tok(c0)] + T_srcs[q * 4:q * 4 + 4], writes=[PSB[4 * slot + q]])
            bt = [PSB[b] for b in banks]
            a = acc[slot]
            P.op("act", lambda e: e.activation(out=a[:, 0:ntok], in_=xp[:, 0:ntok], func=AF.Identity,
                                               scale=cw_ssd[:, j, 2:3], bias=cb_ssd[:, j:j + 1]),
                 reads=bt + [T_vecs], writes=[T_acc[slot]])

        def c2_s2(idx, j, srcT, T_srcs, ntok, dst, T_dst):
            slot = idx % 2
            nq = (ntok + 511) // 512
            bt = [PSB[4 * slot + q] for q in range(nq)]
            xp = psum[:, 4 * slot:4 * slot + 4, :].rearrange("p a b -> p (a b)")
            a = acc[slot]
            for kk, sh in ((0, -2), (1, -1), (3, 1), (4, 2)):
                if sh < 0:
                    o_sl, i_sl = slice(-sh, ntok), slice(0, ntok + sh)
                else:
                    o_sl, i_sl = slice(0, ntok - sh), slice(sh, ntok)
                P.op("dve", lambda e, kk=kk, o_sl=o_sl, i_sl=i_sl: e.scalar_tensor_tensor(
                    out=a[:, o_sl], in0=xp[:, i_sl], scalar=cw_ssd[:, j, kk:kk + 1], in1=a[:, o_sl],
                    op0=ALU.mult, op1=ALU.add),
                    reads=bt + [T_vecs, T_acc[slot]], writes=[T_acc[slot]])
            P.op("act", lambda e: e.activation(out=dst, in_=a[:, 0:ntok], func=AF.Silu),
                 reads=[T_acc[slot]], writes=[T_dst])

        c2_list = [(j, hcT, T_hcT + [T_hcT[1]] * 2, CTX, xbc_ctx[:, j, :], T_xbcc) for j in range(6)]
        c2_list += [(j, hT, T_hT, SEQ, xbcT[:, j, :], T_xbc[j]) for j in range(8)]
        for idx in range(len(c2_list) + 1):
            if idx < len(c2_list):
                c2_s1(idx, *c2_list[idx])
            if idx >= 1:
                c2_s2(idx - 1, *c2_list[idx - 1])

        if stop_after == 'p2':
            P.emit(final_waits)
            return nc
        P.barrier()
        mixT = hT
        T_mixT = [P.toks(f"mixT{j}_", NT) for j in range(8)]
        xsB_tok = carve(A2_OFF, [128, NT, 768], BF16)
        xsB_ctx = carve(A2_OFF + NT * 768 * 2, [128, NCT, 768], BF16)
        T_xsB = P.toks("xsB", NT + NCT)
        prevb_all = carve(A3_OFF, [128, NT, 512], BF16)
        T_prevb = P.toks("prevb", NT)
        BT = xbcT[:, 4:6, :]
        CT = xbcT[:, 6:8, :]
        w_out = carve(WA_OFF, [128, 8, D], BF16)
        T_wout = Tok("wout")
        w_out_v = w_out_d.rearrange("(k p) n -> p k n", p=128)
        for hh in range(2):
            P.op("pool", lambda e, hh=hh: e.dma_start(out=w_out[:, :, hh * 512:(hh + 1) * 512],
                                                     in_=w_out_v[:, :, hh * 512:(hh + 1) * 512]),
                 writes=[T_wout], dma=True, bar=False)
        X1 = Bump(HC_OFF)
        X3 = Bump(A2_OFF + (NT + NCT) * 768 * 2)
        wpool = X1.get([128, 4, 128], BF16)
        T_wpool = Tok("wpool")
        P.op("pool", lambda e: e.dma_start(out=wpool, in_=w_pool_d.rearrange("g c d -> c g d")),
             writes=[T_wpool], dma=True)
        dT = [X1.get([128, 4, 128], BF16) for _ in range(2)]
        ptok = [X1.get([128, 512], BF16) for _ in range(2)]
        T_dT = P.toks("dT", 2)
        T_ptok = P.toks("ptok", 2)
        dtv = X3.get([128, NT + NCT, 16], F32)
        dta = X3.get([128, NT + NCT, 16], F32)
        a_rep = X3.get([128, 16], F32)
        ex_all = X1.get([128, NT + NCT, 5, 16], F32)
        T_dtv, T_dta, T_arep = Tok("dtv"), Tok("dta"), Tok("arep")
        T_ex = P.toks("ex", NT + NCT)
        sstate = [X1.get([128, 512], F32) for _ in range(2)]
        T_sst = [P.toks("sf", 8), P.toks("sb", 8)]
        prevf_bf = [X1.get([128, 512], BF16) for _ in range(2)]
        T_prevf = P.toks("prevf", 2)
        wsc = [X3.get([128, 16], F32) for _ in range(2)]
        T_wsc = P.toks("wsc", 2)
        xd = [X1.get([128, 2, 512], BF16) for _ in range(2)]
        T_xd = P.toks("xd", 2)
        xds = [X1.get([128, 512], BF16) for _ in range(2)]
        T_xds = P.toks("xds", 2)
        CBm = X3.get([128, 4, 2, 128], BF16)
        T_xskip = P.toks("xskip", 2)
        assert X3.off <= A2_END
        T_cbm = P.toks("cbm", 2)
        t1 = [X1.get([128, 512], F32)] * 2
        T_t1 = [Tok("t1")] * 2
        y_sb = [X1.get([128, 512], F32)] * 2
        T_ysb = [Tok("ysb")] * 2
        mix_tok = [X1.get([128, 512], BF16) for _ in range(2)]
        T_mixtok = P.toks("mixtok", 2)
        xskip = [X1.get([128, 512], BF16) for _ in range(2)]
        X2 = Bump(VT_OFF)
        Rm = X2.get([128, 16, 128], F32)
        dec = X2.get([128, 2, 8, 128], BF16)
        Mm = X2.get([128, 16, 128], BF16)
        assert X2.off <= HC_OFF
        T_R, T_dec, T_M = P.toks("R", 2), P.toks("dec", 2), P.toks("M", 2)

        def xsb_tile(i, src, cols, T_srcs, dst, T_dst):
            bank = 5 + (i % 2)
            pt = ps_bf16(bank).rearrange("p (a b) -> p a b", b=128)
            for m in range(6):
                P.op("pe", lambda e, m=m: e.transpose(out=pt[:, m, :], in_=src[:, m, cols], identity=ident_b),
                     reads=T_srcs + [T_const], writes=[PSB[bank]])
            P.op("act", lambda e: e.activation(out=dst, in_=ps_bf16(bank)[:, 0:768], func=AF.Copy),
                 reads=[PSB[bank]], writes=[T_dst])

        for i in range(NCT):
            xsb_tile(i, xbc_ctx, slice(i * 128, (i + 1) * 128), [T_xbcc], xsB_ctx[:, i, :], T_xsB[NT + i])
        for i in range(NT):
            xsb_tile(i, xbcT, slice(i * 128, (i + 1) * 128), T_xbc[0:6], xsB_tok[:, i, :], T_xsB[i])
        for t in T_prevb:
            P.inherit(t, T_xbc[0:4])

        def XS(c):
            return (xsB_tok[:, c, :] if c < NT else xsB_ctx[:, c - NT, :])

        fl = lambda ap: ap.rearrange("p a b -> p (a b)")
        P.op("act", lambda e: e.activation(out=fl(dtv), in_=fl(dtr), func=AF.Exp), reads=[T_dtr], writes=[T_dtv])
        P.op("act", lambda e: e.activation(out=fl(dtv), in_=fl(dtv), func=AF.Ln, bias=1.0), reads=[T_dtv], writes=[T_dtv])
        P.op("act", lambda e: e.activation(out=a_rep, in_=V("a_log"), func=AF.Exp), reads=[T_vecs], writes=[T_arep])
        P.op("dve", lambda e: e.tensor_scalar(out=a_rep, in0=a_rep, scalar1=-1.0, scalar2=None, op0=ALU.mult),
             reads=[T_arep], writes=[T_arep])
        P.op("dve", lambda e: e.tensor_tensor(out=dta, in0=dtv, in1=a_rep.unsqueeze(1).to_broadcast([128, NT + NCT, 16]),
                                              op=ALU.mult), reads=[T_dtv, T_arep], writes=[T_dta])

        NCH = NT + NCT
        for m in range(5):
            lhsT = tri_f[:, m, :] if m < 4 else ones_f
            P.op("pe", lambda e, m=m, lhsT=lhsT: e.matmul(
                psum[:, m, 0:NCH * 16], lhsT=lhsT, rhs=dta.rearrange("p a b -> p (a b)"),
                start=True, stop=True), reads=[T_dta, T_const, T_vecs], writes=[PSB[m]])
            P.op("act", lambda e, m=m: e.activation(
                out=ex_all[:, :, m, :], in_=psum[:, m, 0:NCH * 16].rearrange("p (a b) -> p a b", b=16), func=AF.Exp),
                reads=[PSB[m]], writes=T_ex)

        inv_cnt = V("inv_cnt").rearrange("p (i g) -> p i g", g=4)
        pscale = V("pool_scale")
        def pool_s1(io):
            sl = io % 2
            dbank = 3 + sl
            for g in range(4):
                iis = [ii for ii in range(NT) if (g, io, ii) in _POOL_TABLE]
                for n_, ii in enumerate(iis):
                    idx = _POOL_TABLE[(g, io, ii)]
                    P.op("pe", lambda e, g=g, ii=ii, idx=idx, n_=n_, last=(n_ == len(iis) - 1), dbank=dbank: e.matmul(
                        psum[:, dbank, g * 128:(g + 1) * 128], lhsT=v_tok[:, ii, g * 128:(g + 1) * 128],
                        rhs=pmat[:, idx, :], start=(n_ == 0), stop=last),
                        reads=[T_vtok[ii], T_const], writes=[PSB[dbank]])
            P.op("act", lambda e, sl=sl, dbank=dbank: e.activation(out=fl(dT[sl]), in_=ps_f32(dbank), func=AF.Copy),
                 reads=[PSB[dbank]], writes=[T_dT[sl]])

        def pool_s2(io):
            sl = io % 2
            pbank, tbank = 5 + sl, 7
            for g in range(4):
                P.op("pe", lambda e, g=g, sl=sl, pbank=pbank: e.matmul(
                    psum[:, pbank, g * 128:(g + 1) * 128], lhsT=dT[sl][:, g, :], rhs=wpool[:, g, :],
                    start=True, stop=True), reads=[T_dT[sl], T_wpool], writes=[PSB[pbank]])
            for g in range(4):
                P.op("dve", lambda e, g=g, sl=sl, pbank=pbank, io=io: e.scalar_tensor_tensor(
                    out=ptok[sl][:, g * 128:(g + 1) * 128], in0=psum[:, pbank, g * 128:(g + 1) * 128],
                    scalar=inv_cnt[:, io, g:g + 1], in1=pscale[:, g * 128:(g + 1) * 128], op0=ALU.mult, op1=ALU.mult),
                    reads=[PSB[pbank], T_vecs], writes=[T_ptok[sl]])
            ptb = ps_bf16(tbank).rearrange("p (a b) -> p a b", b=128)
            for g in range(4):
                P.op("pe", lambda e, g=g, sl=sl: e.transpose(out=ptb[:, g, :], in_=ptok[sl][:, g * 128:(g + 1) * 128],
                                                             identity=ident_b),
                     reads=[T_ptok[sl], T_const], writes=[PSB[tbank]])
            P.op("act", lambda e, io=io: e.activation(out=mixT[:, 4:8, io * 128:(io + 1) * 128], in_=ptb[:, 0:4, :],
                                                      func=AF.Copy),
                 reads=[PSB[tbank]], writes=[T_mixT[j][io] for j in range(4, 8)])

        for d_ in range(2):
            P.op("dve", lambda e, d_=d_: e.memset(sstate[d_], 0.0), writes=T_sst[d_])
        dskip = V("d_skip")
        ssm_g = V("ssm_norm")
        LE_b, GE_b = tri_b[:, 0, :], tri_b[:, 3, :]
        LE_f, GT_f, LT_f, GE_f = (tri_f[:, m, :] for m in range(4))
        st_cnt = [0]

        def ls_prep(c, d_, slot, big_eng="dve"):
            ho = 8 * d_
            m_dst = 1 if d_ == 0 else 2
            P.op("dve", lambda e: e.tensor_tensor(out=wsc[slot][:, 0:8], in0=dtv[:, c, ho:ho + 8],
                                                  in1=ex_all[:, c, m_dst, ho:ho + 8], op=ALU.mult),
                 reads=[T_dtv, T_ex[c]], writes=[T_wsc[slot]])
            P.op(big_eng, lambda e: e.tensor_tensor(
                out=xds[slot].rearrange("p (h d) -> p h d", d=64), in0=XS(c)[:, 0:512].rearrange("p (h d) -> p h d", d=64),
                in1=wsc[slot][:, 0:8].unsqueeze(2).to_broadcast([128, 8, 64]), op=ALU.mult),
                reads=[T_xsB[c], T_wsc[slot]], writes=[T_xds[slot]])

        def ls_mm(c, d_, slot, bank):
            for g in range(2):
                P.op("pe", lambda e, g=g: e.matmul(
                    psum[:, bank, g * 256:(g + 1) * 256], lhsT=XS(c)[:, 512 + g * 128:512 + (g + 1) * 128],
                    rhs=xds[slot][:, g * 256:(g + 1) * 256], start=True, stop=True),
                    reads=[T_xsB[c], T_xds[slot]], writes=[PSB[bank]])

        def ls_upd(c, d_, slot, bank):
            ho = 8 * d_
            for h in range(8):
                P.op("dve", lambda e, h=h: e.scalar_tensor_tensor(
                    out=sstate[d_][:, h * 64:(h + 1) * 64], in0=sstate[d_][:, h * 64:(h + 1) * 64],
                    scalar=ex_all[:, c, 4, ho + h:ho + h + 1], in1=psum[:, bank, h * 64:(h + 1) * 64],
                    op0=ALU.mult, op1=ALU.add),
                    reads=[T_sst[d_][h], T_ex[c], PSB[bank]], writes=[T_sst[d_][h]])

        orderB = [NT + 1, NT] + list(range(NT - 1, -1, -1))
        ls_prep(orderB[0], 1, 0, "pool")
        SB = [sstate[1], t1[0]]
        T_SB = [T_sst[1], P.toks("sbB", 8)]
        cur = 0
        for n_, c in enumerate(orderB):
            if c < NT:
                P.op("act", lambda e, c=c, cur=cur: e.activation(out=prevb_all[:, c, :], in_=SB[cur], func=AF.Copy),
                     reads=T_SB[cur], writes=[T_prevb[c]])
            if c != 0:
                bank = n_ % 2
                ls_mm(c, 1, n_ % 2, bank)
                if orderB[n_ + 1] != 0:
                    ls_prep(orderB[n_ + 1], 1, (n_ + 1) % 2, "pool")
                nxt = 1 - cur
                for h in range(8):
                    P.op("dve", lambda e, h=h, c=c, cur=cur, nxt=nxt, bank=bank: e.scalar_tensor_tensor(
                        out=SB[nxt][:, h * 64:(h + 1) * 64], in0=SB[cur][:, h * 64:(h + 1) * 64],
                        scalar=ex_all[:, c, 4, 8 + h:9 + h], in1=psum[:, bank, h * 64:(h + 1) * 64],
                        op0=ALU.mult, op1=ALU.add),
                        reads=[T_SB[cur][h], T_ex[c], PSB[bank]], writes=[T_SB[nxt][h]])
                cur = nxt
        P.inherit(T_t1[0], T_SB[1])
        pool_s1(0)
        for io in range(NT):
            if io + 1 < NT:
                pool_s1(io + 1)
            pool_s2(io)
        for t in T_R + T_dec + T_M:
            P.inherit(t, T_vtok)

        orderF = [NT, NT + 1] + list(range(NT))
        NF = len(orderF)
        YDBS = [4, 7]
        T_cbps = T_trps = PSB[2]

        def F_Ad(n_):
            c = orderF[n_]
            if c >= NT:
                return
            for h in range(8):
                P.op("dve", lambda e, h=h: e.tensor_scalar(
                    out=Rm[:, h, :], in0=LE_f, scalar1=dta[:, c, h:h + 1], scalar2=None,
                    op0=ALU.mult), reads=[T_const, T_dta], writes=[T_R[0]])

        def F_Aa(n_):
            c = orderF[n_]
            if c >= NT:
                return
            for h in range(8):
                P.op("act", lambda e, h=h: e.activation(
                    out=Rm[:, 8 + h, :], in_=GE_f, func=AF.Copy, scale=dta[:, c, 8 + h:9 + h]),
                    reads=[T_const, T_dta], writes=[T_R[1]])

        def F_Dmm(n_):
            c = orderF[n_]
            if c >= NT:
                return
            for hb in range(2):
                for d_ in range(2):
                    lhsT = GT_f if d_ == 0 else LT_f
                    P.op("pe", lambda e, hb=hb, d_=d_, lhsT=lhsT: e.matmul(
                        ps_f32(hb), lhsT=lhsT, rhs=Rm[:, d_ * 8 + hb * 4:d_ * 8 + hb * 4 + 4, :].rearrange("p a b -> p (a b)"),
                        start=(d_ == 0), stop=(d_ == 1)), reads=[T_R[d_], T_const], writes=[PSB[hb]])

        def F_Exp(n_):
            c = orderF[n_]
            sl = n_ % 2
            if c >= NT:
                return
            P.op("act", lambda e: e.activation(
                out=dec[:, sl, :, :].rearrange("p a b -> p (a b)"),
                in_=psum[:, 0:2, :].rearrange("p a b -> p (a b)"), func=AF.Exp),
                reads=[PSB[0], PSB[1]], writes=[T_dec[sl]])

        def F_CB(n_):
            c = orderF[n_]
            sl = n_ % 2
            if c >= NT:
                return
            cols = slice(c * 128, (c + 1) * 128)
            for g in range(2):
                P.op("pe", lambda e, g=g: e.matmul(
                    psum[:, 2, g * 128:(g + 1) * 128], lhsT=BT[:, g, cols], rhs=CT[:, g, cols], start=True, stop=True),
                    reads=T_xbc[4:8], writes=[T_cbps])
            for d_ in range(2):
                mb = LE_b if d_ == 0 else GE_b
                P.op("dve", lambda e, d_=d_, mb=mb: e.tensor_tensor(
                    out=CBm[:, sl * 2 + d_, :, :], in0=psum[:, 2, 0:256].rearrange("p (g l) -> p g l", l=128),
                    in1=mb.unsqueeze(1).to_broadcast([128, 2, 128]), op=ALU.mult),
                    reads=[T_cbps, T_const], writes=[T_cbm[sl]])

        def F_C(n_):
            c = orderF[n_]
            sl = n_ % 2
            if c < NT:
                for d_ in range(2):
                    P.op("dve", lambda e, d_=d_: e.tensor_tensor(
                        out=xd[sl][:, d_, :].rearrange("p (h d) -> p h d", d=64),
                        in0=XS(c)[:, 0:512].rearrange("p (h d) -> p h d", d=64),
                        in1=dtv[:, c, d_ * 8:d_ * 8 + 8].unsqueeze(2).to_broadcast([128, 8, 64]), op=ALU.mult),
                        reads=[T_xsB[c], T_dtv], writes=[T_xd[sl]])
                P.op("dve", lambda e: e.tensor_tensor(
                    out=xskip[sl].rearrange("p (h d) -> p h d", d=64),
                    in0=XS(c)[:, 0:512].rearrange("p (h d) -> p h d", d=64),
                    in1=dskip.unsqueeze(2).to_broadcast([128, 8, 64]), op=ALU.mult),
                    reads=[T_xsB[c], T_vecs], writes=[T_xskip[sl]])
            if c != NT - 1:
                ls_prep(c, 0, sl)
            if c < NT:
                for d_ in range(2):
                    for g in range(2):
                        P.op("dve", lambda e, d_=d_, g=g: e.tensor_tensor(
                            out=Mm[:, d_ * 8 + g * 4:d_ * 8 + g * 4 + 4, :], in0=dec[:, sl, g * 4:g * 4 + 4, :],
                            in1=CBm[:, sl * 2 + d_, g, :].unsqueeze(1).to_broadcast([128, 4, 128]), op=ALU.mult),
                            reads=[T_dec[sl], T_cbm[sl]], writes=[T_M[d_]])

        def F_D(n_):
            c = orderF[n_]
            sl = n_ % 2
            YDB = YDBS[sl]
            if c >= NT:
                return
            for h in range(8):
                for d_ in range(2):
                    P.op("pe", lambda e, h=h, d_=d_: e.matmul(
                        psum[:, YDB, h * 64:(h + 1) * 64], lhsT=Mm[:, d_ * 8 + h, :], rhs=xd[sl][:, d_, h * 64:(h + 1) * 64],
                        start=(d_ == 0), stop=(d_ == 1)), reads=[T_M[d_], T_xd[sl]], writes=[PSB[YDB]])

        def F_E0a(n_):
            c = orderF[n_]
            if c != NT - 1:
                ls_mm(c, 0, n_ % 2, 3)

        def F_E0b(n_):
            c = orderF[n_]
            sl = n_ % 2
            if c != NT - 1:
                ls_upd(c, 0, sl, 3)
            if n_ + 1 < NF and orderF[n_ + 1] < NT:
                nsl = (n_ + 1) % 2
                P.op("act", lambda e: e.activation(out=prevf_bf[nsl], in_=sstate[0], func=AF.Copy),
                     reads=T_sst[0], writes=[T_prevf[nsl]])

        def F_E1a(n_):
            c = orderF[n_]
            sl = n_ % 2
            if c >= NT:
                return
            cols = slice(c * 128, (c + 1) * 128)
            for g in range(2):
                P.op("pe", lambda e, g=g: e.matmul(
                    psum[:, 5, g * 256:(g + 1) * 256], lhsT=CT[:, g, cols], rhs=prevf_bf[sl][:, g * 256:(g + 1) * 256],
                    start=True, stop=True), reads=T_xbc[6:8] + [T_prevf[sl]], writes=[PSB[5]])
                P.op("pe", lambda e, g=g: e.matmul(
                    psum[:, 6, g * 256:(g + 1) * 256], lhsT=CT[:, g, cols], rhs=prevb_all[:, c, g * 256:(g + 1) * 256],
                    start=True, stop=True), reads=T_xbc[6:8] + [T_prevb[c]], writes=[PSB[6]])

        def F_E1b1(n_):
            c = orderF[n_]
            sl = n_ % 2
            if c >= NT:
                return
            for h in range(8):
                P.op("act", lambda e, h=h: e.activation(
                    out=t1[sl][:, h * 64:(h + 1) * 64], in_=psum[:, 5, h * 64:(h + 1) * 64], func=AF.Copy,
                    scale=ex_all[:, c, 0, h:h + 1]), reads=[PSB[5], T_ex[c]], writes=[T_t1[sl]])

        def F_E1b(n_):
            c = orderF[n_]
            sl = n_ % 2
            YDB = YDBS[sl]
            if c >= NT:
                return
            for h in range(8):
                P.op("dve", lambda e, h=h: e.scalar_tensor_tensor(
                    out=y_sb[sl][:, h * 64:(h + 1) * 64], in0=psum[:, 6, h * 64:(h + 1) * 64],
                    scalar=ex_all[:, c, 3, 8 + h:9 + h], in1=t1[sl][:, h * 64:(h + 1) * 64], op0=ALU.mult, op1=ALU.add),
                    reads=[PSB[6], T_ex[c], T_t1[sl]], writes=[T_ysb[sl]])
            P.op("dve", lambda e: e.tensor_tensor(out=t1[sl], in0=y_sb[sl], in1=ps_f32(YDB), op=ALU.add),
                 reads=[T_ysb[sl], PSB[YDB]], writes=[T_t1[sl]])
            P.op("dve", lambda e: e.tensor_tensor(out=t1[sl], in0=t1[sl], in1=xskip[sl], op=ALU.add),
                 reads=[T_t1[sl], T_xskip[sl]], writes=[T_t1[sl]])
            P.op("dve", lambda e: e.tensor_tensor(out=t1[sl], in0=t1[sl], in1=zs[:, c, :], op=ALU.mult),
                 reads=[T_t1[sl], T_zs[c]], writes=[T_t1[sl]])
            col = 32 + c
            P.op("act", lambda e: e.activation(out=mix_tok[sl], in_=t1[sl], func=AF.Square, accum_out=ss[:, col:col + 1]),
                 reads=[T_t1[sl]], writes=[T_mixtok[sl], T_ss[col]])
            P.op("pool", lambda e: e.tensor_scalar(out=ss[:, col:col + 1], in0=ss[:, col:col + 1],
                                                   scalar1=1.0 / 512, scalar2=EPS, op0=ALU.mult, op1=ALU.add),
                 reads=[T_ss[col]], writes=[T_ss[col]])
            P.op("pool", lambda e: e.tensor_tensor(out=rstd[:, col:col + 1], in0=ss[:, col:col + 1],
                                                   in1=V("neghalf")[:, 0:1], op=ALU.pow),
                 reads=[T_ss[col], T_vecs], writes=[T_ss[col]])

        def F_E2(n_):
            c = orderF[n_]
            sl = n_ % 2
            if c >= NT:
                return
            cols = slice(c * 128, (c + 1) * 128)
            col = 32 + c
            P.op("dve", lambda e: e.scalar_tensor_tensor(
                out=mix_tok[sl], in0=t1[sl], scalar=rstd[:, col:col + 1], in1=ssm_g, op0=ALU.mult, op1=ALU.mult),
                reads=[T_t1[sl], T_ss[col], T_vecs], writes=[T_mixtok[sl]])
            ptb = ps_bf16(2).rearrange("p (a b) -> p a b", b=128)
            for m in range(4):
                P.op("pe", lambda e, m=m: e.transpose(out=ptb[:, 4 + m, :], in_=mix_tok[sl][:, m * 128:(m + 1) * 128],
                                                      identity=ident_b),
                     reads=[T_mixtok[sl], T_const], writes=[T_trps])
            P.op("act", lambda e: e.activation(out=mixT[:, 0:4, cols], in_=ptb[:, 4:8, :], func=AF.Copy),
                 reads=[T_trps], writes=[T_mixT[j][c] for j in range(4)])

        for it in range(-3, NF):
            for fn, idx in ((F_E0b, it), (F_E1a, it), (F_CB, it + 1), (F_Dmm, it + 2), (F_C, it + 1), (F_D, it + 1),
                            (F_E1b1, it), (F_Exp, it + 2), (F_Aa, it + 3), (F_E1b, it), (F_Ad, it + 3), (F_E2, it),
                            (F_E0a, it + 1)):
                if 0 <= idx < NF:
                    fn(idx)

        if stop_after == 'p3':
            P.emit(final_waits)
            return nc
        P.barrier()
        h2T = hT
        Y4 = Bump(HC_OFF)
        xt4 = [Y4.get([128, D], F32) for _ in range(3)]
        xh4 = [Y4.get([128, D], BF16) for _ in range(2)]
        junk4 = Y4.get([128, D], BF16)
        tmp4 = [Y4.get([128, D], F32) for _ in range(2)]
        T_xt4, T_xh4, T_junk4, T_tmp4 = P.toks("xt4", 3), P.toks("xh4", 2), Tok("junk4"), P.toks("tmp4", 2)
        T_x1d = P.toks("x1d", NT)
        w_down = carve(A2_OFF, [128, NFF, D], BF16)
        T_wdown = P.toks("wdown", 2)
        w_down_v = w_down_d.rearrange("(k p) n -> p k n", p=128)
        for hh in range(2):
            for ch in range(2):
                P.op("pool", lambda e, hh=hh, ch=ch: e.dma_start(
                    out=w_down[:, hh * 11:(hh + 1) * 11, ch * 512:(ch + 1) * 512],
                    in_=w_down_v[:, hh * 11:(hh + 1) * 11, ch * 512:(ch + 1) * 512]),
                    writes=[T_wdown[hh]], dma=True, bar=False)

        def rms_cols(src_ap, n, col, T_src, junk_ap, T_j):
            P.op("act", lambda e: e.activation(out=junk_ap, in_=src_ap, func=AF.Square, accum_out=ss[:, col:col + 1]),
                 reads=T_src, writes=[T_j, T_ss[col]])
            P.op("pool", lambda e: e.tensor_scalar(out=ss[:, col:col + 1], in0=ss[:, col:col + 1], scalar1=1.0 / n,
                                                   scalar2=EPS, op0=ALU.mult, op1=ALU.add),
                 reads=[T_ss[col]], writes=[T_ss[col]])
            P.op("pool", lambda e: e.tensor_tensor(out=rstd[:, col:col + 1], in0=ss[:, col:col + 1],
                                                   in1=V("neghalf")[:, 0:1], op=ALU.pow),
                 reads=[T_ss[col], T_vecs], writes=[T_ss[col]])

        def p4_A(i):
            cols = slice(i * 128, (i + 1) * 128)
            s3, s2 = i % 3, i % 2
            yb = [2 * s2, 2 * s2 + 1]
            for hh in range(2):
                for k in range(8):
                    P.op("pe", lambda e, hh=hh, k=k, cols=cols, bank=yb[hh]: e.matmul(
                        ps_f32(bank), lhsT=mixT[:, k, cols], rhs=w_out[:, k, hh * 512:(hh + 1) * 512],
                        start=(k == 0), stop=(k == 7)), reads=[T_mixT[k][i], T_wout], writes=[PSB[yb[hh]]])
            P.op("sp", lambda e, i=i, s3=s3: e.dma_start(out=xt4[s3], in_=x_d[i * 128:(i + 1) * 128, :]),
                 writes=[T_xt4[s3]], dma=True)

        def p4_B(i):
            s3, s2 = i % 3, i % 2
            yb = [2 * s2, 2 * s2 + 1]
            yx = psum[:, yb[0]:yb[0] + 2, :].rearrange("p a b -> p (a b)")
            ybt = [PSB[yb[0]], PSB[yb[1]]]
            rms_cols(yx, D, i, ybt, junk4, T_junk4)
            P.op("dve", lambda e, s2=s2, i=i, yx=yx: e.scalar_tensor_tensor(
                out=tmp4[s2], in0=yx, scalar=rstd[:, i:i + 1], in1=gg_rep[:, 0, :], op0=ALU.mult, op1=ALU.mult),
                reads=ybt + [T_ss[i], T_gg], writes=[T_tmp4[s2]])
            P.op("dve", lambda e, s2=s2, s3=s3: e.tensor_tensor(out=xt4[s3], in0=xt4[s3], in1=tmp4[s2], op=ALU.add),
                 reads=[T_xt4[s3], T_tmp4[s2]], writes=[T_xt4[s3]])
            o4 = P.op("sp", lambda e, i=i, s3=s3: e.dma_start(out=out_d[i * 128:(i + 1) * 128, :], in_=xt4[s3]),
                      reads=[T_xt4[s3]], writes=[T_x1d[i]], dma=True)
            if stop_after == 'p4':
                final_waits.append(o4)

        def p4_C(i):
            s3, s2 = i % 3, i % 2
            col = 16 + i
            rms_cols(xt4[s3], D, col, [T_xt4[s3]], junk4, T_junk4)
            P.op("act", lambda e, s2=s2, s3=s3, col=col: e.activation(out=xh4[s2], in_=xt4[s3], func=AF.Copy,
                                                                      scale=rstd[:, col:col + 1]),
                 reads=[T_xt4[s3], T_ss[col]], writes=[T_xh4[s2]])

        def p4_D(i):
            cols = slice(i * 128, (i + 1) * 128)
            s2 = i % 2
            bank = 5 + s2
            pt = ps_bf16(bank).rearrange("p (a b) -> p a b", b=128)
            for j in range(8):
                P.op("pe", lambda e, j=j, s2=s2, pt=pt: e.transpose(out=pt[:, j, :], in_=xh4[s2][:, j * 128:(j + 1) * 128],
                                                                    identity=ident_b),
                     reads=[T_xh4[s2], T_const], writes=[PSB[bank]])
            for j in range(8):
                P.op("dve", lambda e, j=j, pt=pt, cols=cols: e.tensor_scalar(
                    out=h2T[:, j, cols], in0=pt[:, j, :], scalar1=mod[:, j, 4:5], scalar2=mod[:, j, 5:6],
                    op0=ALU.mult, op1=ALU.add), reads=[PSB[bank], T_mod], writes=[T_mixT[j][i]])

        for n_ in range(NT + 3):
            if n_ < NT:
                p4_A(n_)
            if 0 <= n_ - 1 < NT:
                p4_B(n_ - 1)
            if 0 <= n_ - 2 < NT:
                p4_C(n_ - 2)
            if 0 <= n_ - 3 < NT:
                p4_D(n_ - 3)

        if stop_after != 'p4':
            P.barrier()
            T_h2 = [[T_mixT[j][i] for j in range(8)] for i in range(NT)]
            Z5 = Bump(A2_OFF + NFF * D * 2)
            ACTW = 1152
            actb = Z5.get([128, NFF, ACTW], BF16)
            T_act = P.toks("act", NFF)
            NWU = 4
            wu = [Z5.get([128, 8, 2, 128], BF16) for _ in range(NWU)]
            T_wu = [P.toks("wuv", NWU), P.toks("wug", NWU)]
            gc = [Z5.get([128, 512], F32) for _ in range(3)]
            vc = [Z5.get([128, 512], F32) for _ in range(3)]
            sg = [Z5.get([128, 512], F32) for _ in range(3)]
            T_gc, T_vc, T_sg = P.toks("gc", 3), P.toks("vc", 3), P.toks("sg", 3)
            xt5 = [Z5.get([128, D], F32) for _ in range(2)]
            tmp5 = [Z5.get([128, D], F32) for _ in range(2)]
            junk5 = Z5.get([128, D], BF16)
            T_xt5, T_tmp5, T_junk5 = P.toks("xt5", 2), P.toks("tmp5", 2), Tok("junk5")
            w_up_v = w_up_d.rearrange("(k p) (two n) -> p k two n", p=128, two=2)
            cw_ffn = V("cw_ffn").rearrange("p (j k) -> p j k", k=3)
            cb_ffn = V("cb_ffn")
            wcnt = [0]
            NPAIR = 2 * NFF

            def load_pair(qi):
                j, ws = qi % NFF, qi % NWU
                for two in range(2):
                    P.op("pool", lambda e, two=two: e.dma_start(out=wu[ws][:, :, two, :],
                                                               in_=w_up_v[:, :, two, j * 128:(j + 1) * 128]),
                         writes=[T_wu[two][ws]], dma=True, bar=False)

            for qi in range(3):
                load_pair(qi)
            for hf in range(2):
                if hf == 0:
                    base, tiles_h = 0, list(range(0, 7))
                    wins = [(0, 512, 0, 511), (510, 1022, 511, 1021)]
                else:
                    base, tiles_h = 896, list(range(7, NT))
                    wins = [(1020, 1532, 1021, 1531), (1530, 2042, 1531, 2041), (2040, 2048, 2041, 2048)]
                    P.op("act", lambda e: e.activation(out=actb[:, :, 0:125], in_=actb[:, :, 896:1021], func=AF.Copy),
                         reads=T_act, writes=T_act)
                for j in range(NFF):
                    qi = hf * NFF + j
                    ws = qi % NWU
                    if qi + 3 < NPAIR:
                        load_pair(qi + 3)
                    for (rl, rh, ol, oh) in wins:
                        sl = wcnt[0] % 3
                        wcnt[0] += 1
                        n = rh - rl
                        vb, gb = 2 * sl, 2 * sl + 1
                        tiles = sorted(set(range(rl // 128, (rh - 1) // 128 + 1)))
                        rd = [t for ti in tiles for t in T_h2[ti]]
                        for k in range(8):
                            P.op("pe", lambda e, k=k, ws=ws, vb=vb, n=n, rl=rl, rh=rh: e.matmul(
                                psum[:, vb, 0:n], lhsT=wu[ws][:, k, 0, :], rhs=h2T[:, k, rl:rh], start=(k == 0), stop=(k == 7)),
                                reads=[T_wu[0][ws]] + rd, writes=[PSB[vb]])
                            P.op("pe", lambda e, k=k, ws=ws, gb=gb, n=n, rl=rl, rh=rh: e.matmul(
                                psum[:, gb, 0:n], lhsT=wu[ws][:, k, 1, :], rhs=h2T[:, k, rl:rh], start=(k == 0), stop=(k == 7)),
                                reads=[T_wu[1][ws]] + rd, writes=[PSB[gb]])
                        no = oh - ol
                        o0 = ol - rl
                        la = 1 if ol == 0 else 0
                        rb = 1 if oh == SEQ else 0
                        specs = ((gb, NFF + j, gc[sl], T_gc[sl]), (vb, j, vc[sl], T_vc[sl]))
                        for (bank, jj, buf, T_b) in specs:
                            P.op("act", lambda e, bank=bank, jj=jj, buf=buf, o0=o0, no=no: e.activation(
                                out=buf[:, 0:no], in_=psum[:, bank, o0:o0 + no], func=AF.Identity,
                                scale=cw_ffn[:, jj, 1:2], bias=cb_ffn[:, jj:jj + 1]),
                                reads=[PSB[bank], T_vecs], writes=[T_b])
                        for (bank, jj, buf, T_b) in specs:
                            P.op("dve", lambda e, bank=bank, jj=jj, buf=buf, o0=o0, no=no, la=la: e.scalar_tensor_tensor(
                                out=buf[:, la:no], in0=psum[:, bank, o0 - 1 + la:o0 - 1 + no], scalar=cw_ffn[:, jj, 0:1],
                                in1=buf[:, la:no], op0=ALU.mult, op1=ALU.add),
                                reads=[PSB[bank], T_vecs, T_b], writes=[T_b])
                        for (bank, jj, buf, T_b) in specs:
                            P.op("dve", lambda e, bank=bank, jj=jj, buf=buf, o0=o0, no=no, rb=rb: e.scalar_tensor_tensor(
                                out=buf[:, 0:no - rb], in0=psum[:, bank, o0 + 1:o0 + 1 + no - rb], scalar=cw_ffn[:, jj, 2:3],
                                in1=buf[:, 0:no - rb], op0=ALU.mult, op1=ALU.add),
                                reads=[PSB[bank], T_vecs, T_b], writes=[T_b])
                        P.op("act", lambda e, sl=sl, no=no: e.activation(out=sg[sl][:, 0:no], in_=gc[sl][:, 0:no], func=AF.Silu),
                             reads=[T_gc[sl]], writes=[T_sg[sl]])
                        P.op("pool", lambda e, sl=sl, no=no, j=j, ol=ol, base=base: e.tensor_tensor(
                            out=actb[:, j, ol - base:ol - base + no], in0=vc[sl][:, 0:no], in1=sg[sl][:, 0:no], op=ALU.mult),
                            reads=[T_vc[sl], T_sg[sl]], writes=[T_act[j]])
                for i in tiles_h:
                    c0a = i * 128 - base
                    s2 = i % 2
                    fb = [6, 7] if s2 == 0 else [4, 5]
                    fx = psum[:, fb[0]:fb[0] + 2, :].rearrange("p a b -> p (a b)")
                    for hh in range(2):
                        for k in range(NFF):
                            P.op("pe", lambda e, hh=hh, k=k, c0a=c0a, bank=fb[hh]: e.matmul(
                                ps_f32(bank), lhsT=actb[:, k, c0a:c0a + 128], rhs=w_down[:, k, hh * 512:(hh + 1) * 512],
                                start=(k == 0), stop=(k == NFF - 1)),
                                reads=[T_act[k], T_wdown[k // 11]], writes=[PSB[fb[hh]]])
                    fbt = [PSB[fb[0]], PSB[fb[1]]]
                    P.op("sp", lambda e, i=i, s2=s2: e.dma_start(out=xt5[s2], in_=out_d[i * 128:(i + 1) * 128, :]),
                         reads=[T_x1d[i]], writes=[T_xt5[s2]], dma=True)
                    col = 32 + i
                    rms_cols(fx, D, col, fbt, junk5, T_junk5)
                    P.op("dve", lambda e, s2=s2, col=col, fx=fx: e.scalar_tensor_tensor(
                        out=tmp5[s2], in0=fx, scalar=rstd[:, col:col + 1], in1=gg_rep[:, 1, :], op0=ALU.mult, op1=ALU.mult),
                        reads=fbt + [T_ss[col], T_gg], writes=[T_tmp5[s2]])
                    P.op("dve", lambda e, s2=s2: e.tensor_tensor(out=xt5[s2], in0=xt5[s2], in1=tmp5[s2], op=ALU.add),
                         reads=[T_xt5[s2], T_tmp5[s2]], writes=[T_xt5[s2]])
                    o = P.op("sp", lambda e, i=i, s2=s2: e.dma_start(out=out_d[i * 128:(i + 1) * 128, :], in_=xt5[s2]),
                             reads=[T_xt5[s2]], writes=[T_x1d[i]], dma=True)
                    final_waits.append(o)


        def dump(name, src, toks):
            o = P.op("sp", lambda e: e.dma_start(out=dbg_d[name], in_=src), reads=toks, dma=True)
            final_waits.append(o)

        if "hT" in dbg_d:
            dump("hT", hT, T_hT)
        if "hcT" in dbg_d:
            dump("hcT", hcT, T_hcT)
        if "ada" in dbg_d:
            dump("ada", ada, [T_ada])
        if "mod" in dbg_d:
            dump("mod", mod, [T_mod])
        if "zs" in dbg_d:
            dump("zs", zs, T_zs)
        if "v_tok" in dbg_d:
            dump("v_tok", v_tok, T_vtok)
        if "dtr" in dbg_d:
            dump("dtr", dtr, [T_dtr])
        if "xbcT" in dbg_d:
            dump("xbcT", xbcT, T_xbc)
        if "xbc_ctx" in dbg_d:
            dump("xbc_ctx", xbc_ctx, [T_xbcc])
        if "h2T" in dbg_d:
            dump("h2T", h2T, [t for l in T_mixT for t in l])
        if "mixT" in dbg_d:
            dump("mixT", mixT, [t for l in T_mixT for t in l])
        if "prevb" in dbg_d:
            dump("prevb", prevb_all, T_prevb)
        if "dtv" in dbg_d:
            dump("dtv", dtv, [T_dtv])
        if "ex_all" in dbg_d:
            dump("ex_all", ex_all, T_ex)
        if "gg_rep" in dbg_d:
            dump("gg_rep", gg_rep, [T_gg])

        P.emit(final_waits)
    return nc


def make_in_maps(inp):
    inp = {k: np.asarray(v) for k, v in inp.items()}
    ident = np.eye(128, dtype=np.float32)
    tri = _tri()
    shared = {
        "w_ada": np.ascontiguousarray(inp["w_ada"][0]),
        "w_in": np.ascontiguousarray(inp["w_in"][0]),
        "w_pool": np.ascontiguousarray(inp["w_pool"][0]),
        "w_out": np.ascontiguousarray(inp["w_out"][0]),
        "w_up": np.ascontiguousarray(inp["w_up"][0]),
        "w_down": np.ascontiguousarray(inp["w_down"][0]),
        "ident": ident, "tri": tri, "pmat": _POOL_MATS,
    }
    maps = []
    for b in range(8):
        m = dict(shared)
        m["x"] = np.ascontiguousarray(inp["x"][b])
        m["ctx"] = np.ascontiguousarray(inp["ctx"][b])
        m["vecs"] = _host_vecs(inp, b)
        maps.append(m)
    return maps


def kernel(**inputs):
    nc = build_nc()
    maps = make_in_maps(inputs)
    res = run_bass_kernel_spmd(nc, maps, core_ids=list(range(8)))
    return np.stack([r["out"] for r in res.results], axis=0).astype(np.float32)
```
